# Optimizing a Trainium2 kernel written in Bass

```python
import math
import jax, jax.numpy as jnp
from jax import lax
import numpy as np

D_MODEL = 1024
BATCH = 16
SEQ = 4096
DEPTH = 2

CTX_LEN = 256
GRID_W = 64
W_BRANCH = D_MODEL
N_BRANCH = 3
LRU_BLOCKS = 8
LRU_BW = W_BRANCH // LRU_BLOCKS
LRU_C = 8.0
CONV_W = 4
DIFF_HEAD_DIM = 64
DIFF_V_DIM = 2 * DIFF_HEAD_DIM
DIFF_HEADS = W_BRANCH // DIFF_V_DIM
SGU_GROUPS = 8
CHUNK = 128
Q_BLOCK = 128
ROPE_BASE = 10000.0
LN_EPS = 1e-5
RMS_EPS = 1e-5
D_IN = 9 * W_BRANCH + N_BRANCH * D_MODEL
SPLIT_POINTS = tuple(W_BRANCH * i for i in range(1, 10))
f32 = jnp.float32

kernel_name = "hybrid_rglru_diffattn_sgu_diffusion_block"


def layer_norm(x, g, b):
    xf = x.astype(f32)
    mu = jnp.mean(xf, axis=-1, keepdims=True)
    var = jnp.mean(jnp.square(xf - mu), axis=-1, keepdims=True)
    return ((xf - mu) * lax.rsqrt(var + LN_EPS) * g.astype(f32) + b.astype(f32)).astype(x.dtype)


def modulate(h, shift, scale):
    return h * (1 + scale) + shift


def split_proj(h, w_in, b_in):
    z = jnp.einsum('btd,de->bte', h, w_in) + b_in
    return jnp.split(z, SPLIT_POINTS, axis=-1)


def centred_dwconv(x, w, b):
    T = x.shape[1]
    left = CONV_W // 2
    xp = jnp.pad(x, ((0, 0), (left, CONV_W - 1 - left), (0, 0)))
    y = b + xp[:, 0:T] * w[0]
    for k in range(1, CONV_W):
        y = y + xp[:, k:k + T] * w[k]
    return y


def block_diag(x, w):
    B, T, _ = x.shape
    xg = x.reshape(B, T, LRU_BLOCKS, LRU_BW)
    return jnp.einsum('btgi,gij->btgj', xg, w).reshape(B, T, W_BRANCH)


def _lin_comb(e1, e2):
    a1, b1 = e1
    a2, b2 = e2
    return a1 * a2, a2 * b1 + b2


def rglru_scan(xc, w_a, b_a, w_x, b_x, lam, h0, reverse):
    r = jax.nn.sigmoid(block_diag(xc, w_a) + b_a).astype(f32)
    i = jax.nn.sigmoid(block_diag(xc, w_x) + b_x).astype(f32)
    log_a = -LRU_C * r * jax.nn.softplus(-lam.astype(f32))
    a = jnp.exp(log_a)
    bt = jnp.sqrt(-jnp.expm1(2.0 * log_a)) * (i * xc.astype(f32))
    if reverse:
        a, bt = a[:, ::-1], bt[:, ::-1]
    bt = bt.at[:, 0].add(a[:, 0] * h0)
    _, h = lax.associative_scan(_lin_comb, (a, bt), axis=1)
    h_last = h[:, -1]
    if reverse:
        h = h[:, ::-1]
    return h.astype(xc.dtype), h_last


def axial_rope_tables(n_tokens):
    rows = n_tokens // GRID_W
    row = jnp.repeat(jnp.arange(rows), GRID_W).astype(f32)
    col = jnp.tile(jnp.arange(GRID_W), rows).astype(f32)
    nf = DIFF_HEAD_DIM // 4
    freqs = ROPE_BASE ** (-jnp.arange(nf, dtype=f32) / nf)
    ang = jnp.concatenate([row[:, None] * freqs, col[:, None] * freqs], axis=-1)
    return jnp.cos(ang), jnp.sin(ang)


def apply_axial_rope(x, cos, sin):
    T = x.shape[1]
    nf = DIFF_HEAD_DIM // 4
    xr = x.reshape(x.shape[:-1] + (2, 2, nf))
    x1, x2 = xr[..., 0, :], xr[..., 1, :]
    c = cos.reshape(T, 1, 1, 2, nf).astype(x.dtype)
    s = sin.reshape(T, 1, 1, 2, nf).astype(x.dtype)
    out = jnp.stack([x1 * c - x2 * s, x1 * s + x2 * c], axis=-2)
    return out.reshape(x.shape)


def diff_attn_core(q, k, v, lam):
    s = jnp.einsum('bqhmd,bkhmd->bhmqk', q.astype(f32), k.astype(f32)) * (DIFF_HEAD_DIM ** -0.5)
    p = jax.nn.softmax(s, axis=-1)
    w = p[:, :, 0] - lam * p[:, :, 1]
    return jnp.einsum('bhqk,bkhe->bqhe', w, v.astype(f32)).astype(v.dtype)


def latent_diff_attention(q, k_all, v_all, lam):
    B, T = q.shape[:2]
    nblk = T // Q_BLOCK
    qb = q.reshape((B, nblk, Q_BLOCK) + q.shape[2:]).swapaxes(0, 1)
    out = lax.map(lambda qi: diff_attn_core(qi, k_all, v_all, lam), qb)
    return out.swapaxes(0, 1).reshape(B, T, DIFF_HEADS, DIFF_V_DIM)


def diff_head_out(o, g, lam_init):
    of = o.astype(f32)
    of = of * lax.rsqrt(jnp.mean(jnp.square(of), axis=-1, keepdims=True) + RMS_EPS)
    of = of * g.astype(f32) * (1.0 - lam_init)
    B, T = o.shape[:2]
    return of.reshape(B, T, W_BRANCH).astype(o.dtype)


def spatial_gating(u, v, ln_g, ln_b, w_s, b_s):
    B, T, W = v.shape
    vn = layer_norm(v, ln_g, ln_b)
    vc = vn.reshape(B, T // CHUNK, CHUNK, SGU_GROUPS, W // SGU_GROUPS)
    s = jnp.einsum('gpq,bnqgc->bnpgc', w_s, vc) + b_s.T[:, :, None]
    return u * s.reshape(B, T, W)


def merge_branches(h_a, g_a, attn, g_b, sgu, g_c, gates, w_branch, w_out):
    y_a = (h_a * jax.nn.silu(g_a)) @ w_branch[0]
    y_b = (attn * jax.nn.silu(g_b)) @ w_branch[1]
    y_c = (sgu * jax.nn.silu(g_c)) @ w_branch[2]
    s_a, s_b, s_c = jnp.split(jax.nn.sigmoid(gates), N_BRANCH, axis=-1)
    return (s_a * y_a + s_b * y_b + s_c * y_c) @ w_out


def setup_inputs(seed: int = 0) -> dict:
    key = jax.random.key(seed)
    ks = jax.random.split(key, 26)
    L = DEPTH
    beta = (8.0 * DEPTH) ** -0.25

    def nrm(k, shape, s):
        return jax.random.normal(k, shape, f32) * s

    u = jax.random.uniform(ks[14], (L, 2, W_BRANCH), f32, 0.9, 0.999)
    a = u ** (1.0 / LRU_C)
    return {
        "x": nrm(ks[0], (BATCH, SEQ, D_MODEL), 1.0),
        "c": nrm(ks[1], (BATCH, D_MODEL), 1.0),
        "ctx": nrm(ks[2], (BATCH, CTX_LEN, D_MODEL), 1.0),
        "c_ctx": nrm(ks[3], (D_MODEL,), 1.0),
        "w_ada": nrm(ks[4], (L, D_MODEL, 3 * D_MODEL), 0.5 * D_MODEL ** -0.5),
        "b_ada": nrm(ks[5], (L, 3 * D_MODEL), 0.01),
        "w_in": nrm(ks[6], (L, D_MODEL, D_IN), D_MODEL ** -0.5),
        "b_in": nrm(ks[7], (L, D_IN), 0.01),
        "conv_w": nrm(ks[8], (L, CONV_W, W_BRANCH), CONV_W ** -0.5),
        "conv_b": nrm(ks[9], (L, W_BRANCH), 0.01),
        "lru_wa": nrm(ks[10], (L, 2, LRU_BLOCKS, LRU_BW, LRU_BW), LRU_BW ** -0.5),
        "lru_ba": nrm(ks[11], (L, 2, W_BRANCH), 0.01),
        "lru_wx": nrm(ks[12], (L, 2, LRU_BLOCKS, LRU_BW, LRU_BW), LRU_BW ** -0.5),
        "lru_bx": nrm(ks[13], (L, 2, W_BRANCH), 0.01),
        "lru_lambda": jnp.log(a) - jnp.log1p(-a),
        "diff_lambda": nrm(ks[15], (L, 4, DIFF_HEAD_DIM), 0.1),
        "diff_norm_g": 1.0 + nrm(ks[16], (L, DIFF_V_DIM), 0.02),
        "sgu_ln_g": 1.0 + nrm(ks[17], (L, W_BRANCH), 0.02),
        "sgu_ln_b": nrm(ks[18], (L, W_BRANCH), 0.01),
        "sgu_w": nrm(ks[19], (L, SGU_GROUPS, CHUNK, CHUNK), 0.5 * CHUNK ** -0.5),
        "sgu_b": 1.0 + nrm(ks[20], (L, SGU_GROUPS, CHUNK), 0.02),
        "w_branch": nrm(ks[21], (L, N_BRANCH, W_BRANCH, D_MODEL), beta * W_BRANCH ** -0.5),
        "w_out": nrm(ks[22], (L, D_MODEL, D_MODEL), beta * D_MODEL ** -0.5),
        "ln_g": 1.0 + nrm(ks[23], (L, D_MODEL), 0.02),
        "ln_b": nrm(ks[24], (L, D_MODEL), 0.01),
    }


def reference(x, c, ctx, c_ctx, w_ada, b_ada, w_in, b_in, conv_w, conv_b,
              lru_wa, lru_ba, lru_wx, lru_bx, lru_lambda, diff_lambda, diff_norm_g,
              sgu_ln_g, sgu_ln_b, sgu_w, sgu_b, w_branch, w_out, ln_g, ln_b):
    alpha = (2.0 * DEPTH) ** 0.25
    B, T, _ = x.shape
    C = ctx.shape[1]
    cos, sin = axial_rope_tables(T)
    zeros = jnp.zeros((B, W_BRANCH), f32)
    for l in range(DEPTH):
        lam_init = 0.8 - 0.6 * math.exp(-0.3 * l)
        lq1, lk1, lq2, lk2 = diff_lambda[l].astype(f32)
        lam = jnp.exp(jnp.sum(lq1 * lk1)) - jnp.exp(jnp.sum(lq2 * lk2)) + lam_init
        sh_x, sc_x, gt_x = jnp.split((jax.nn.silu(c) @ w_ada[l] + b_ada[l])[:, None, :], 3, axis=-1)
        sh_c, sc_c, gt_c = jnp.split(jax.nn.silu(c_ctx) @ w_ada[l] + b_ada[l], 3, axis=-1)

        pc = split_proj(modulate(ctx, sh_c, sc_c), w_in[l], b_in[l])
        ca = centred_dwconv(pc[0], conv_w[l], conv_b[l])
        hc_f, hf_last = rglru_scan(ca, lru_wa[l, 0], lru_ba[l, 0], lru_wx[l, 0], lru_bx[l, 0],
                                   lru_lambda[l, 0], zeros, False)
        hc_b, hb_last = rglru_scan(ca, lru_wa[l, 1], lru_ba[l, 1], lru_wx[l, 1], lru_bx[l, 1],
                                   lru_lambda[l, 1], zeros, True)
        kc = pc[3].reshape(B, C, DIFF_HEADS, 2, DIFF_HEAD_DIM)
        vc = pc[4].reshape(B, C, DIFF_HEADS, DIFF_V_DIM)

        px = split_proj(modulate(x, sh_x, sc_x), w_in[l], b_in[l])
        xa = centred_dwconv(px[0], conv_w[l], conv_b[l])
        hx_f, _ = rglru_scan(xa, lru_wa[l, 0], lru_ba[l, 0], lru_wx[l, 0], lru_bx[l, 0],
                             lru_lambda[l, 0], hf_last, False)
        hx_b, _ = rglru_scan(xa, lru_wa[l, 1], lru_ba[l, 1], lru_wx[l, 1], lru_bx[l, 1],
                             lru_lambda[l, 1], hb_last, True)
        qx = apply_axial_rope(px[2].reshape(B, T, DIFF_HEADS, 2, DIFF_HEAD_DIM), cos, sin)
        kx = apply_axial_rope(px[3].reshape(B, T, DIFF_HEADS, 2, DIFF_HEAD_DIM), cos, sin)
        vx = px[4].reshape(B, T, DIFF_HEADS, DIFF_V_DIM)
        k_all = jnp.concatenate([kx, kc], axis=1)
        v_all = jnp.concatenate([vx, vc], axis=1)
        attn_x = diff_head_out(latent_diff_attention(qx, k_all, v_all, lam), diff_norm_g[l], lam_init)
        sgu_x = spatial_gating(px[6], px[7], sgu_ln_g[l], sgu_ln_b[l], sgu_w[l], sgu_b[l])
        y_x = merge_branches(hx_f + hx_b, px[1], attn_x, px[5], sgu_x, px[8], px[9],
                             w_branch[l], w_out[l])
        x_new = layer_norm(alpha * x + gt_x * y_x, ln_g[l], ln_b[l])

        if l < DEPTH - 1:
            qc = pc[2].reshape(B, C, DIFF_HEADS, 2, DIFF_HEAD_DIM)
            attn_c = diff_head_out(diff_attn_core(qc, kc, vc, lam), diff_norm_g[l], lam_init)
            sgu_c = spatial_gating(pc[6], pc[7], sgu_ln_g[l], sgu_ln_b[l], sgu_w[l], sgu_b[l])
            y_c = merge_branches(hc_f + hc_b, pc[1], attn_c, pc[5], sgu_c, pc[8], pc[9],
                                 w_branch[l], w_out[l])
            ctx = layer_norm(alpha * ctx + gt_c * y_c, ln_g[l], ln_b[l])
        x = x_new
    return x
```

```python
import math
from contextlib import ExitStack
import numpy as np
import ml_dtypes
import concourse.bass as bass
import concourse.mybir as mybir
from concourse.bass_utils import run_bass_kernel_spmd

F32 = mybir.dt.float32
BF16 = mybir.dt.bfloat16
ALU = mybir.AluOpType
AF = mybir.ActivationFunctionType
AX = mybir.AxisListType

D = 1024
KC = 8
DIN = 12 * D
LN_EPS = 1e-5
RMS_EPS = 1e-5
ENGS = ("pe", "act", "dve", "pool", "sp")
NDMASEM = 12
SEM_LIMIT = 30000


class Res:
    __slots__ = ("name", "writers", "readers", "war")

    def __init__(self, name=""):
        self.name = name
        self.writers = []
        self.readers = []
        self.war = []


class Rec:
    def __init__(self):
        self.calls = []

    def __getattr__(self, name):
        def f(*a, **k):
            self.calls.append((name, a, k))
            return None
        return f


class Op:
    __slots__ = ("eng", "fn", "deps", "is_dma", "signal", "sem", "val")

    def __init__(self, eng, fn, is_dma):
        self.eng = eng
        if fn is not None:
            r = Rec()
            fn(r)
            fn = r.calls
            assert len(fn) > 0
        self.fn = fn
        self.deps = []
        self.is_dma = is_dma
        self.signal = False
        self.sem = None
        self.val = None


class Sched:
    def __init__(self):
        self.q = {e: [] for e in ENGS}

    def _add(self, eng, fn, reads, writes, partial, is_dma):
        op = Op(eng, fn, is_dma)
        deps = {}
        for r in reads:
            for w in r.writers:
                deps[id(w)] = w
        for w_ in writes:
            if id(w_) in partial:
                if w_.readers:
                    w_.war = w_.readers
                    w_.writers = []
                    w_.readers = []
                for x in w_.war:
                    deps[id(x)] = x
            else:
                for x in w_.writers:
                    deps[id(x)] = x
                for x in w_.readers:
                    deps[id(x)] = x
                for x in w_.war:
                    deps[id(x)] = x
        for r in reads:
            r.readers.append(op)
        for w_ in writes:
            if id(w_) in partial:
                w_.writers.append(op)
            else:
                w_.writers = [op]
                w_.readers = []
                w_.war = [op]
        dl = []
        for d in deps.values():
            if d is op:
                continue
            if (not d.is_dma) and (not is_dma) and d.eng == eng and eng == "pe":
                continue
            dl.append(d)
            d.signal = True
        op.deps = dl
        self.q[eng].append(op)
        return op

    def op(self, eng, fn, reads=(), writes=(), partial=()):
        return self._add(eng, fn, list(reads), list(writes), set(id(p) for p in partial), False)

    def dma(self, eng, fn, reads=(), writes=(), partial=()):
        op = self._add(eng, fn, list(reads), list(writes), set(id(p) for p in partial), True)
        op.signal = True
        return op

    def barrier(self):
        tails = []
        for e in ENGS:
            ql = self.q[e]
            nd = 0
            got_c = False
            for op in reversed(ql):
                if op.fn is None:
                    break
                if op.is_dma:
                    if nd < NDMASEM:
                        tails.append(op)
                        nd += 1
                elif not got_c:
                    tails.append(op)
                    op.signal = True
                    got_c = True
                if got_c and nd >= NDMASEM:
                    break
        for e in ENGS:
            op = Op(e, None, False)
            op.deps = [t for t in tails]
            self.q[e].append(op)

    def emit(self, block, sems, dma_sems):
        for e in ENGS:
            cnt = 0
            dcnt = [0] * NDMASEM
            k = 0
            for op in self.q[e]:
                if op.fn is None:
                    continue
                if op.is_dma:
                    s = k % NDMASEM
                    k += 1
                    dcnt[s] += 16
                    op.sem = ("d", e, s)
                    op.val = dcnt[s]
                elif op.signal:
                    op.sem = ("c", e, cnt // SEM_LIMIT)
                    op.val = cnt % SEM_LIMIT + 1
                    cnt += 1

        def semobj(key):
            if key[0] == "c":
                return sems[key[1]][key[2]]
            return dma_sems[key[1]][key[2]]

        def run(e, engobj):
            seen = {}
            prev_on_sem = {}
            for op in self.q[e]:
                waits = {}
                for d in op.deps:
                    if d.sem is None:
                        continue
                    if waits.get(d.sem, 0) < d.val:
                        waits[d.sem] = d.val
                if op.is_dma:
                    pv = prev_on_sem.get(op.sem)
                    if pv is not None and waits.get(op.sem, 0) < pv:
                        waits[op.sem] = pv
                    prev_on_sem[op.sem] = op.val
                for key, v in waits.items():
                    if seen.get(key, 0) >= v:
                        continue
                    seen[key] = v
                    engobj.wait_ge(semobj(key), v)
                if op.fn is None:
                    continue
                ins = None
                for (nm, a, k) in op.fn:
                    ins = getattr(engobj, nm)(*a, **k)
                if op.is_dma:
                    ins.then_inc(semobj(op.sem), 16)
                elif op.signal:
                    ins.then_inc(semobj(op.sem), 1)
            last = {}
            for op in self.q[e]:
                if op.is_dma:
                    last[op.sem] = op.val
            for key, v in last.items():
                if seen.get(key, 0) < v:
                    engobj.wait_ge(semobj(key), v)

        @block.sync
        def _(eng):
            run("sp", eng)

        @block.gpsimd
        def _(eng):
            run("pool", eng)

        @block.scalar
        def _(eng):
            run("act", eng)

        @block.vector
        def _(eng):
            run("dve", eng)

        @block.tensor
        def _(eng):
            run("pe", eng)

    def sem_gens(self):
        out = {}
        for e in ENGS:
            n = sum(1 for op in self.q[e] if (op.fn is not None and not op.is_dma and op.signal))
            out[e] = n // SEM_LIMIT + 1
        return out


class TL:
    def __init__(self, ap, name):
        self.ap = ap
        self.res = Res(name)

    def v3(self, a):
        return self.ap.rearrange("p (a b) -> p a b", a=a)


def rev(ap_):
    apl = [list(x) for x in ap_.ap]
    n = apl[-1][1]
    s = apl[-1][0]
    off = ap_.offset + s * (n - 1)
    apl[-1][0] = -s
    return bass.AP(ap_.tensor, off, apl)


def build_program(T, C, L, NSEQ, dbg=False):
    NT = C + T
    NCH = NT // 128
    NV = NSEQ + 1
    alpha = (2.0 * L) ** 0.25
    nc = bass.Bass("TRN2", target_bir_lowering=False)
    S = Sched()

    def din(name, shape, dt=F32):
        return nc.dram_tensor(name, list(shape), dt, kind="ExternalInput").ap()

    def dscr(name, shape, dt):
        return nc.dram_tensor(name, list(shape), dt, kind=("ExternalOutput" if dbg else "Internal")).ap()

    x_d = din("x", [NSEQ, T, D])
    ctx_d = din("ctx", [NSEQ, C, D])
    cvec_d = din("cvec", [NV, D])
    w_ada = din("w_ada", [L, D, 3 * D])
    b_ada = din("b_ada", [L, 3 * D])
    w_in = din("w_in", [L, D, DIN])
    b_in = din("b_in", [L, DIN])
    conv_w = din("conv_w", [L, 4, D])
    conv_b = din("conv_b", [L, D])
    lru_wa = din("lru_wa", [L, 2, 8, 128, 128])
    lru_ba = din("lru_ba", [L, 2, D])
    lru_wx = din("lru_wx", [L, 2, 8, 128, 128])
    lru_bx = din("lru_bx", [L, 2, D])
    lru_lambda = din("lru_lambda", [L, 2, D])
    diff_lambda = din("diff_lambda", [L, 4, 64])
    diff_norm_g = din("diff_norm_g", [L, 128])
    sgu_ln_g = din("sgu_ln_g", [L, D])
    sgu_ln_b = din("sgu_ln_b", [L, D])
    sgu_w = din("sgu_w", [L, 8, 128, 128])
    sgu_b = din("sgu_b", [L, 8, 128])
    w_branch = din("w_branch", [L, 3, D, D])
    w_out = din("w_out", [L, D, D])
    ln_g = din("ln_g", [L, D])
    ln_b = din("ln_b", [L, D])
    identb_d = din("identb", [128, 128], BF16)
    identf_d = din("identf", [128, 128])
    permf_d = din("permf", [128, 128])
    cos_d = din("cosT", [128, T])
    sin_d = din("sinT", [128, T])
    sel_d = din("sel", [NV, NV * 128])
    out_d = nc.dram_tensor("out", [NSEQ, T, D], F32, kind="ExternalOutput").ap()

    Win_bf = [dscr("win_bf%d" % l, [128, KC, DIN], BF16) for l in range(L)]
    Wbr_bf = [dscr("wbr_bf%d" % l, [128, 3, KC, D], BF16) for l in range(L)]
    Wout_bf = [dscr("wout_bf%d" % l, [128, KC, D], BF16) for l in range(L)]
    Wlru_bf = [dscr("wlru_bf%d" % l, [128, 2, 2, 8, 128], BF16) for l in range(L)]
    hT_scr = [dscr("hT_scr%d" % s, [128, KC, NT], BF16) for s in range(NSEQ)]
    A_scr = [dscr("A_scr%d" % s, [KC, 128, NT], F32) for s in range(NSEQ)]
    hA_scr = [dscr("hA_scr%d" % s, [KC, 128, NT], BF16) for s in range(NSEQ)]
    qT_scr = [dscr("qT_scr%d" % s, [KC, 128, NT], BF16) for s in range(NSEQ)]
    kT_scr = [dscr("kT_scr%d" % s, [KC, 128, NT], BF16) for s in range(NSEQ)]
    V_scr = [dscr("V_scr%d" % s, [NT, D], BF16) for s in range(NSEQ)]
    at_scr = [dscr("at_scr%d" % s, [KC, 128, NT], BF16) for s in range(NSEQ)]
    sg_scr = [dscr("sg_scr%d" % s, [KC, 128, NT], BF16) for s in range(NSEQ)]
    x1_scr = dscr("x1_scr", [NSEQ, T, D], F32)
    c1_scr = dscr("c1_scr", [NSEQ, C, D], F32)

    ARENA = 53000
    es = ExitStack()
    arena = es.enter_context(nc.sbuf_tensor("arena", [128, ARENA], F32))
    ps = es.enter_context(nc.psum_tensor("ps", [128, 8 * 512], F32))
    PB = [Res("psb%d" % i) for i in range(8)]

    def bank(i, n=512, p0=0, p1=128):
        return ps[p0:p1, i * 512:i * 512 + n]

    st = {"off": 0, "base": 0, "pb": 0}

    def alloc(words, name):
        a = arena[:, st["off"]:st["off"] + words]
        st["off"] += words
        assert st["off"] <= ARENA, ("SBUF arena overflow", name, st["off"])
        return TL(a, name)

    def alloc_bf(n, name):
        t = alloc((n + 1) // 2, name)
        t.ap = t.ap.bitcast(BF16)
        return t

    def stage_begin():
        S.barrier()
        st["off"] = st["base"]

    def nbank(n=1):
        if n == 2 and st["pb"] % 2 == 1:
            st["pb"] += 1
        b = st["pb"] % 8
        st["pb"] += n
        return b

    identb = alloc_bf(128, "identb")
    identf = alloc(128, "identf")
    permf = alloc(128, "permf")
    onesb = alloc_bf(128, "onesb")
    onesm = alloc_bf(128, "onesm")
    onesf = alloc(128, "onesf")
    sel = alloc(NV * 128, "sel")
    fmA = alloc(128, "fmA")
    fmB = alloc(128, "fmB")
    modrow = alloc(3 * D, "modrow")
    small = alloc(64, "small")
    cpt = alloc(32, "cpt")
    WsT = alloc_bf(8 * 128, "WsT")
    st["base"] = st["off"]

    S.dma("sp", lambda e: e.dma_start(out=identb.ap, in_=identb_d), writes=[identb.res])
    S.dma("sp", lambda e: e.dma_start(out=identf.ap, in_=identf_d), writes=[identf.res])
    S.dma("sp", lambda e: e.dma_start(out=permf.ap, in_=permf_d), writes=[permf.res])
    S.dma("sp", lambda e: e.dma_start(out=sel.ap[0:NV, :], in_=sel_d), writes=[sel.res])
    S.op("pool", lambda e: e.memset(onesb.ap, 1.0), writes=[onesb.res])
    S.op("pool", lambda e: e.memset(onesm.ap, 1.0 / 128.0), writes=[onesm.res])
    S.op("pool", lambda e: e.memset(onesf.ap, 1.0), writes=[onesf.res])

    for l in range(L):
        for j in range(12):
            S.dma("pool", lambda e, l=l, j=j: e.dma_start(
                out=Win_bf[l][:, :, j * D:(j + 1) * D],
                in_=w_in[l, :, j * D:(j + 1) * D].rearrange("(k p) e -> p k e", p=128)))
        for j in range(3):
            S.dma("pool", lambda e, l=l, j=j: e.dma_start(
                out=Wbr_bf[l][:, j, :, :], in_=w_branch[l, j].rearrange("(k p) e -> p k e", p=128)))
        S.dma("pool", lambda e, l=l: e.dma_start(
            out=Wout_bf[l], in_=w_out[l].rearrange("(k p) e -> p k e", p=128)))
        for dr in range(2):
            S.dma("pool", lambda e, l=l, dr=dr: e.dma_start(
                out=Wlru_bf[l][:, dr, 0, :, :], in_=lru_wa[l, dr].rearrange("g i j -> i g j")))
            S.dma("pool", lambda e, l=l, dr=dr: e.dma_start(
                out=Wlru_bf[l][:, dr, 1, :, :], in_=lru_wx[l, dr].rearrange("g i j -> i g j")))

    def mm_group(out_ap, pairs, reads, wres, partial=False, start=True, stop=True):
        def fn(e):
            n = len(pairs)
            ins = None
            for i, (lt, rh) in enumerate(pairs):
                ins = e.matmul(out_ap, lt, rh, start=(start and i == 0), stop=(stop and i == n - 1))
            return ins
        S.op("pe", fn, reads=reads, writes=[wres], partial=([wres] if partial else []))

    def bcast_load(tile, src_row):
        S.dma("sp", lambda e: e.dma_start(out=tile.ap, in_=src_row.partition_broadcast(128)), writes=[tile.res])

    def layer_norm_rows(src, dst_fn, gbc, bbc, stats, mv, rstd, eps):
        sap, sres = src
        S.op("dve", lambda e: e.bn_stats(stats.ap[:, 0:6], sap[:, 0:512]), reads=[sres], writes=[stats.res], partial=[stats.res])
        S.op("dve", lambda e: e.bn_stats(stats.ap[:, 6:12], sap[:, 512:1024]), reads=[sres], writes=[stats.res], partial=[stats.res])
        S.op("dve", lambda e: e.bn_aggr(mv.ap[:, 0:2], stats.ap[:, 0:12]), reads=[stats.res], writes=[mv.res])
        S.op("act", lambda e: e.activation(out=rstd.ap[:, 0:1], in_=mv.ap[:, 1:2], func=AF.Sqrt, bias=float(eps)),
             reads=[mv.res], writes=[rstd.res])
        S.op("dve", lambda e: e.reciprocal(rstd.ap[:, 0:1], rstd.ap[:, 0:1]), reads=[rstd.res], writes=[rstd.res])
        S.op("dve", lambda e: e.tensor_scalar(sap, sap, mv.ap[:, 0:1], rstd.ap[:, 0:1], ALU.subtract, ALU.mult),
             reads=[sres, mv.res, rstd.res], writes=[sres])
        S.op("pool", lambda e: e.tensor_tensor(sap, sap, gbc.ap, ALU.mult), reads=[sres, gbc.res], writes=[sres])
        dst_ap, dst_res = dst_fn
        S.op("pool", lambda e: e.tensor_tensor(dst_ap, sap, bbc.ap, ALU.add), reads=[sres, bbc.res], writes=[dst_res])

    for l in range(L):
        last = (l == L - 1)
        lam_init = 0.8 - 0.6 * math.exp(-0.3 * l)
        stage_begin()
        vsA = alloc(128, "vsA")
        vsB = alloc(128, "vsB")
        S.dma("sp", lambda e, l=l: e.dma_start(out=vsA.ap[0:96, :], in_=b_in[l].rearrange("(r c) -> r c", c=128)),
              writes=[vsA.res], partial=[vsA.res])
        S.dma("sp", lambda e, l=l: e.dma_start(out=vsA.ap[96:128, :], in_=conv_w[l].rearrange("t (k c) -> (t k) c", c=128)),
              writes=[vsA.res], partial=[vsA.res])
        S.op("pool", lambda e: e.memset(vsB.ap, 0.0), writes=[vsB.res])
        rows = [(0, 8, conv_b[l].rearrange("(r c) -> r c", c=128)),
                (8, 16, lru_ba[l].rearrange("d (k c) -> (d k) c", c=128)),
                (24, 16, lru_bx[l].rearrange("d (k c) -> (d k) c", c=128)),
                (40, 16, lru_lambda[l].rearrange("d (k c) -> (d k) c", c=128)),
                (56, 1, diff_norm_g[l:l + 1, :]),
                (57, NV * 8, cvec_d.rearrange("v (k c) -> (v k) c", c=128))]
        for (r0, n, src) in rows:
            S.dma("sp", lambda e, r0=r0, n=n, src=src: e.dma_start(out=vsB.ap[r0:r0 + n, :], in_=src),
                  writes=[vsB.res], partial=[vsB.res])
        for (vs, fm) in ((vsA, fmA), (vsB, fmB)):
            b = nbank()
            S.op("pe", lambda e, vs=vs, b=b: e.transpose(bank(b, 128), vs.ap, identf.ap),
                 reads=[vs.res, identf.res], writes=[PB[b]])
            S.op("dve", lambda e, fm=fm, b=b: e.tensor_copy(fm.ap, bank(b, 128)), reads=[PB[b]], writes=[fm.res])

        dl = alloc(256, "dl")
        bcast_load(dl, diff_lambda[l].rearrange("a b -> (a b)"))
        pr = alloc(128, "pr")
        S.op("dve", lambda e: e.tensor_tensor(pr.ap[:, 0:64], dl.ap[:, 0:64], dl.ap[:, 64:128], ALU.mult),
             reads=[dl.res], writes=[pr.res], partial=[pr.res])
        S.op("dve", lambda e: e.tensor_tensor(pr.ap[:, 64:128], dl.ap[:, 128:192], dl.ap[:, 192:256], ALU.mult),
             reads=[dl.res], writes=[pr.res], partial=[pr.res])
        S.op("dve", lambda e: e.reduce_sum(small.ap[:, 0:1], pr.ap[:, 0:64], axis=AX.X), reads=[pr.res], writes=[small.res], partial=[small.res])
        S.op("dve", lambda e: e.reduce_sum(small.ap[:, 1:2], pr.ap[:, 64:128], axis=AX.X), reads=[pr.res], writes=[small.res], partial=[small.res])
        sm2 = alloc(8, "sm2")
        S.op("act", lambda e: e.activation(out=sm2.ap[:, 0:2], in_=small.ap[:, 0:2], func=AF.Exp), reads=[small.res], writes=[sm2.res])
        sm3 = alloc(8, "sm3")
        S.op("dve", lambda e: e.tensor_tensor(sm3.ap[:, 0:1], sm2.ap[:, 1:2], sm2.ap[:, 0:1], ALU.subtract), reads=[sm2.res], writes=[sm3.res])
        neglam = alloc(8, "neglam")
        S.op("dve", lambda e, li=lam_init: e.tensor_scalar(neglam.ap[:, 0:1], sm3.ap[:, 0:1], float(-li), None, ALU.add),
             reads=[sm3.res], writes=[neglam.res], partial=[neglam.res])
        S.op("dve", lambda e, li=lam_init: e.tensor_scalar(neglam.ap[:, 1:2], fmB.ap[:, 56:57], float(1.0 - li), None, ALU.mult),
             reads=[fmB.res], writes=[neglam.res], partial=[neglam.res])
        S.op("dve", lambda e: e.tensor_copy(small.ap[:, 4:6], neglam.ap[:, 0:2]), reads=[neglam.res], writes=[small.res], partial=[small.res])
        lt1 = alloc(16, "lt1")
        S.op("act", lambda e: e.activation(out=lt1.ap, in_=fmB.ap[:, 40:56], func=AF.Exp, scale=-1.0), reads=[fmB.res], writes=[lt1.res])
        lt2 = alloc(16, "lt2")
        S.op("act", lambda e: e.activation(out=lt2.ap, in_=lt1.ap, func=AF.Ln, bias=1.0), reads=[lt1.res], writes=[lt2.res])
        S.op("dve", lambda e: e.tensor_scalar(cpt.ap[:, 0:16], lt2.ap, -8.0, None, ALU.mult), reads=[lt2.res], writes=[cpt.res], partial=[cpt.res])
        S.op("dve", lambda e: e.tensor_scalar(cpt.ap[:, 16:32], lt2.ap, -16.0, None, ALU.mult), reads=[lt2.res], writes=[cpt.res], partial=[cpt.res])

        scv = alloc(NV * 8, "scv")
        S.op("act", lambda e: e.activation(out=scv.ap.rearrange("p (k v) -> p v k", v=NV),
                                           in_=fmB.ap[:, 57:57 + NV * 8].rearrange("p (v k) -> p v k", k=8), func=AF.Silu),
             reads=[fmB.res], writes=[scv.res])
        bada = alloc(3 * D, "bada")
        S.dma("sp", lambda e, l=l: e.dma_start(out=bada.ap[0:NV, :], in_=b_ada[l].partition_broadcast(NV)), writes=[bada.res])
        wab = [alloc(8 * 512, "wab%d" % i) for i in range(2)]
        for ct in range(6):
            wb = wab[ct % 2]
            S.dma("sp", lambda e, l=l, ct=ct, wb=wb: e.dma_start(
                out=wb.v3(8), in_=w_ada[l, :, ct * 512:(ct + 1) * 512].rearrange("(k p) e -> p k e", p=128)), writes=[wb.res])
            b = nbank()
            mm_group(bank(b, 512, 0, NV), [(scv.ap[:, k * NV:(k + 1) * NV], wb.v3(8)[:, k, :]) for k in range(8)],
                     [scv.res, wb.res], PB[b])
            S.op("dve", lambda e, b=b, ct=ct: e.tensor_tensor(modrow.ap[0:NV, ct * 512:(ct + 1) * 512], bank(b, 512, 0, NV),
                                                              bada.ap[0:NV, ct * 512:(ct + 1) * 512], ALU.add),
                 reads=[PB[b], bada.res], writes=[modrow.res], partial=[modrow.res])
        S.op("dve", lambda e: e.tensor_scalar(modrow.ap[0:NV, D:2 * D], modrow.ap[0:NV, D:2 * D], 1.0, None, ALU.add),
             reads=[modrow.res], writes=[modrow.res])

        swt = [alloc(128, "swt%d" % i) for i in range(2)]
        for g in range(8):
            sw = swt[g % 2]
            S.dma("sp", lambda e, l=l, g=g, sw=sw: e.dma_start(out=sw.ap, in_=sgu_w[l, g]), writes=[sw.res])
            b = nbank()
            S.op("pe", lambda e, sw=sw, b=b: e.transpose(bank(b, 128), sw.ap, identf.ap), reads=[sw.res, identf.res], writes=[PB[b]])
            S.op("act", lambda e, g=g, b=b: e.activation(out=WsT.ap[:, g * 128:(g + 1) * 128], in_=bank(b, 128), func=AF.Copy),
                 reads=[PB[b]], writes=[WsT.res], partial=[WsT.res])

        def make_bc(tile, v, part):
            for half in range(2):
                b = nbank()
                mm_group(bank(b, 512), [(sel.ap[0:NV, v * 128:(v + 1) * 128],
                                         modrow.ap[0:NV, part * D + half * 512: part * D + (half + 1) * 512])],
                         [sel.res, modrow.res], PB[b])
                S.op("act", lambda e, b=b, half=half: e.activation(out=tile.ap[:, half * 512:(half + 1) * 512], in_=bank(b, 512), func=AF.Copy),
                     reads=[PB[b]], writes=[tile.res], partial=[tile.res])

        for s in range(NSEQ):
            tiles = [(0, C, True)] + [(C + i * 512, 512, False) for i in range(T // 512)]
            stage_begin()
            sc1_s = alloc(D, "sc1_s"); sh_s = alloc(D, "sh_s")
            sc1_c = alloc(D, "sc1_c"); sh_c = alloc(D, "sh_c")
            make_bc(sh_s, s, 0); make_bc(sc1_s, s, 1)
            make_bc(sh_c, NSEQ, 0); make_bc(sc1_c, NSEQ, 1)
            bV = alloc(D, "bV"); bCv = alloc(D, "bCv"); lng = alloc(D, "lng"); lnb = alloc(D, "lnb"); sgub = alloc(D, "sgub")
            bcast_load(bV, b_in[l, 4 * D:5 * D])
            bcast_load(bCv, b_in[l, 7 * D:8 * D])
            bcast_load(lng, sgu_ln_g[l])
            bcast_load(lnb, sgu_ln_b[l])
            bcast_load(sgub, sgu_b[l].rearrange("g p -> (g p)"))
            xin = [alloc(D, "xin%d" % i) for i in range(2)]
            htmp = alloc(D, "htmp")
            hmod = [alloc_bf(D, "hmod%d" % i) for i in range(2)]
            hT = [alloc_bf(8 * 512, "hT%d" % i) for i in range(2)]
            wbuf = [alloc_bf(8 * 512, "wbuf%d" % i) for i in range(3)]
            cosb = [alloc(512, "cos%d" % i) for i in range(2)]
            sinb = [alloc(512, "sin%d" % i) for i in range(2)]
            axst = [alloc(512, "axst%d" % i) for i in range(3)]
            q32 = [alloc(512, "q32_%d" % i) for i in range(3)]
            rt1 = [alloc(512, "rt1_%d" % i) for i in range(2)]
            rt2 = [alloc(512, "rt2_%d" % i) for i in range(2)]
            qkst = [alloc_bf(512, "qkst%d" % i) for i in range(4)]
            vst = alloc_bf(4 * D, "vst")
            ust = alloc_bf(8 * 512, "ust")
            cv32 = [alloc(D, "cv32_%d" % i) for i in range(4)]
            vn = [alloc_bf(D, "vn%d" % i) for i in range(4)]
            s32 = [alloc(512, "s32_%d" % i) for i in range(2)]
            sgust = alloc_bf(8 * 512, "sgust")
            stats = alloc(12, "stats"); mv = alloc(2, "mv"); rstd = alloc(2, "rstd")
            cnt = {"w": 0, "ax": 0, "q": 0, "qk": 0, "x": 0}

            def build_hT(ti):
                tok0, N, is_ctx = tiles[ti]
                NS = N // 128
                hTt = hT[ti % 2]
                hT3 = hTt.v3(8)
                sc1 = sc1_c if is_ctx else sc1_s
                sh = sh_c if is_ctx else sh_s
                for sub in range(NS):
                    xi = xin[cnt["x"] % 2]; hm = hmod[cnt["x"] % 2]; cnt["x"] += 1
                    if l == 0:
                        src = ctx_d[s, sub * 128:(sub + 1) * 128, :] if is_ctx else x_d[s, tok0 - C + sub * 128: tok0 - C + (sub + 1) * 128, :]
                    else:
                        src = c1_scr[s, sub * 128:(sub + 1) * 128, :] if is_ctx else x1_scr[s, tok0 - C + sub * 128: tok0 - C + (sub + 1) * 128, :]
                    S.dma("sp", lambda e, xi=xi, src=src: e.dma_start(out=xi.ap, in_=src), writes=[xi.res])
                    S.op("dve", lambda e, xi=xi, sc1=sc1: e.tensor_tensor(htmp.ap, xi.ap, sc1.ap, ALU.mult),
                         reads=[xi.res, sc1.res], writes=[htmp.res])
                    S.op("pool", lambda e, hm=hm, sh=sh: e.tensor_tensor(hm.ap, htmp.ap, sh.ap, ALU.add),
                         reads=[htmp.res, sh.res], writes=[hm.res])
                    b = nbank()
                    pT = bank(b).bitcast(BF16)

                    def trf(e, hm=hm, pT=pT):
                        ins = None
                        for k in range(8):
                            ins = e.transpose(pT[:, k * 128:(k + 1) * 128], hm.ap[:, k * 128:(k + 1) * 128], identb.ap)
                        return ins
                    S.op("pe", trf, reads=[hm.res, identb.res], writes=[PB[b]])
                    S.op("act", lambda e, hT3=hT3, pT=pT, sub=sub: e.activation(
                        out=hT3[:, :, sub * 128:(sub + 1) * 128], in_=pT.rearrange("p (k t) -> p k t", k=8), func=AF.Copy),
                        reads=[PB[b]], writes=[hTt.res], partial=[hTt.res])
                S.dma("pool", lambda e, hT3=hT3, tok0=tok0, N=N, s=s: e.dma_start(out=hT_scr[s][:, :, tok0:tok0 + N], in_=hT3[:, :, 0:N]),
                      reads=[hTt.res])

            build_hT(0)
            for ti, (tok0, N, is_ctx) in enumerate(tiles):
                NS = N // 128
                hTt = hT[ti % 2]
                hT3 = hTt.v3(8)
                if not is_ctx:
                    cb = cosb[ti % 2]; sb_ = sinb[ti % 2]
                    S.dma("sp", lambda e, cb=cb, tok0=tok0: e.dma_start(out=cb.ap, in_=cos_d[:, tok0 - C:tok0 - C + 512]), writes=[cb.res])
                    S.dma("sp", lambda e, sb_=sb_, tok0=tok0: e.dma_start(out=sb_.ap, in_=sin_d[:, tok0 - C:tok0 - C + 512]), writes=[sb_.res])

                def load_w(col0):
                    wb = wbuf[cnt["w"] % 3]; cnt["w"] += 1
                    S.dma("sp", lambda e, wb=wb, col0=col0, l=l: e.dma_start(out=wb.v3(8), in_=Win_bf[l][:, :, col0:col0 + 512]), writes=[wb.res])
                    return wb

                def fm_slice(sl, evac):
                    for piece in range(2):
                        wb = load_w(sl * D + piece * 512)
                        w3 = wb.v3(8)
                        for ec in range(4):
                            c = piece * 4 + ec
                            b = nbank()
                            mm_group(bank(b, N), [(w3[:, k, ec * 128:(ec + 1) * 128], hT3[:, k, 0:N]) for k in range(8)],
                                     [wb.res, hTt.res], PB[b])
                            evac(c, b)

                def ev_ax(c, b):
                    a = axst[cnt["ax"] % 3]; cnt["ax"] += 1
                    S.op("act", lambda e, a=a, b=b, c=c: e.activation(out=a.ap[:, 0:N], in_=bank(b, N), func=AF.Identity,
                                                                        bias=fmA.ap[:, c:c + 1]),
                         reads=[PB[b], fmA.res], writes=[a.res])
                    S.dma("pool", lambda e, a=a, c=c: e.dma_start(out=A_scr[s][c, :, tok0:tok0 + N], in_=a.ap[:, 0:N]), reads=[a.res])

                def make_ev_qk(sl, dst):
                    pend = []

                    def rope_part(q, t1, t2, qs, c):
                        b2 = nbank()
                        mm_group(bank(b2, N), [(permf.ap, q.ap)], [permf.res, q.res], PB[b2])
                        S.op("dve", lambda e: e.tensor_tensor(t1.ap, q.ap, cb.ap, ALU.mult),
                             reads=[q.res, cb.res], writes=[t1.res])
                        S.op("dve", lambda e: e.tensor_tensor(t2.ap, bank(b2, N), sb_.ap, ALU.mult),
                             reads=[PB[b2], sb_.res], writes=[t2.res])
                        S.op("pool", lambda e: e.tensor_tensor(qs.ap, t1.ap, t2.ap, ALU.add),
                             reads=[t1.res, t2.res], writes=[qs.res])
                        S.dma("pool", lambda e: e.dma_start(out=dst[s][c, :, tok0:tok0 + N], in_=qs.ap[:, 0:N]), reads=[qs.res])

                    def flush():
                        while pend:
                            rope_part(*pend.pop(0))

                    def ev(c, b):
                        qs = qkst[cnt["qk"] % 4]; cnt["qk"] += 1
                        bcol = sl * 8 + c
                        if is_ctx:
                            S.op("act", lambda e, qs=qs, b=b: e.activation(out=qs.ap[:, 0:N], in_=bank(b, N), func=AF.Identity,
                                                                           bias=fmA.ap[:, bcol:bcol + 1]),
                                 reads=[PB[b], fmA.res], writes=[qs.res])
                            S.dma("pool", lambda e, qs=qs, c=c: e.dma_start(out=dst[s][c, :, tok0:tok0 + N], in_=qs.ap[:, 0:N]), reads=[qs.res])
                        else:
                            i2 = cnt["q"] % 3; cnt["q"] += 1
                            q = q32[i2]; t1 = rt1[i2 % 2]; t2 = rt2[i2 % 2]
                            S.op("act", lambda e, q=q, b=b: e.activation(out=q.ap, in_=bank(b, N), func=AF.Identity,
                                                                         bias=fmA.ap[:, bcol:bcol + 1]),
                                 reads=[PB[b], fmA.res], writes=[q.res])
                            flush()
                            pend.append((q, t1, t2, qs, c))
                    ev.flush = flush
                    return ev
                def qk_slices(mid):
                    if not (is_ctx and last):
                        evq = make_ev_qk(2, qT_scr)
                        fm_slice(2, evq)
                        evq.flush()
                    if mid is not None:
                        mid()
                    evk = make_ev_qk(3, kT_scr)
                    fm_slice(3, evk)
                    evk.flush()

                def tm_slice(sl, evac):
                    for half in range(2):
                        wb = load_w(sl * D + half * 512)
                        w3 = wb.v3(8)
                        for sub in range(NS):
                            b = nbank()
                            mm_group(bank(b, 512), [(hT3[:, k, sub * 128:(sub + 1) * 128], w3[:, k, :]) for k in range(8)],
                                     [wb.res, hTt.res], PB[b])
                            evac(sub, half, b)

                vst3 = vst.v3(4)

                def ev_v(sub, half, b):
                    S.op("dve", lambda e, sub=sub, half=half, b=b: e.tensor_tensor(
                        vst3[:, sub, half * 512:(half + 1) * 512], bank(b, 512), bV.ap[:, half * 512:(half + 1) * 512], ALU.add),
                        reads=[PB[b], bV.res], writes=[vst.res], partial=[vst.res])
                def do_v():
                    tm_slice(4, ev_v)
                    S.dma("pool", lambda e, NS=NS: e.dma_start(out=V_scr[s][tok0:tok0 + N, :].rearrange("(s p) e -> p s e", p=128),
                                                                in_=vst3[:, 0:NS, :]), reads=[vst.res])
                ust3 = ust.v3(8)

                def ev_u(c, b):
                    S.op("act", lambda e, c=c, b=b: e.activation(out=ust3[:, c, 0:N], in_=bank(b, N), func=AF.Identity,
                                                                 bias=fmA.ap[:, 48 + c:48 + c + 1]),
                         reads=[PB[b], fmA.res], writes=[ust.res], partial=[ust.res])

                def ev_cv(sub, half, b):
                    cv = cv32[sub]
                    S.op("dve", lambda e, cv=cv, half=half, b=b: e.tensor_tensor(
                        cv.ap[:, half * 512:(half + 1) * 512], bank(b, 512), bCv.ap[:, half * 512:(half + 1) * 512], ALU.add),
                        reads=[PB[b], bCv.res], writes=[cv.res], partial=[cv.res])
                sg3 = sgust.v3(8)

                def do_cv():
                    tm_slice(7, ev_cv)

                def do_ln():
                    for sub in range(NS):
                        cv = cv32[sub]; vnt = vn[sub]
                        layer_norm_rows((cv.ap, cv.res), (vnt.ap, vnt.res), lng, lnb, stats, mv, rstd, LN_EPS)

                def do_sgu():
                  for sub in range(NS):
                    vnt = vn[sub]
                    for gh in range(2):
                        b = nbank()

                        def sgf(e, vnt=vnt, b=b, gh=gh):
                            ins = None
                            for gi in range(4):
                                g = gh * 4 + gi
                                ins = e.matmul(bank(b, 512)[:, gi * 128:(gi + 1) * 128], vnt.ap[:, g * 128:(g + 1) * 128],
                                               WsT.ap[:, g * 128:(g + 1) * 128], start=True, stop=True)
                            return ins
                        S.op("pe", sgf, reads=[vnt.res, WsT.res], writes=[PB[b]])
                        sx = s32[gh]
                        S.op("dve", lambda e, sx=sx, b=b, gh=gh: e.tensor_tensor(sx.ap, bank(b, 512), sgub.ap[:, gh * 512:(gh + 1) * 512], ALU.add),
                             reads=[PB[b], sgub.res], writes=[sx.res])
                        S.op("pool", lambda e, sx=sx, gh=gh, sub=sub: e.tensor_tensor(
                            sg3[:, gh * 4:(gh + 1) * 4, sub * 128:(sub + 1) * 128],
                            ust3[:, gh * 4:(gh + 1) * 4, sub * 128:(sub + 1) * 128],
                            sx.ap.rearrange("p (g t) -> p g t", g=4), ALU.mult),
                            reads=[sx.res, ust.res], writes=[sgust.res], partial=[sgust.res])
                  S.dma("pool", lambda e, N=N, tok0=tok0: e.dma_start(out=sg_scr[s].rearrange("c p t -> p c t")[:, :, tok0:tok0 + N],
                                                                     in_=sg3[:, :, 0:N]), reads=[sgust.res])

                full = not (is_ctx and last)
                if full:
                    do_cv()
                    fm_slice(6, ev_u)
                fm_slice(0, ev_ax)
                if full:
                    do_ln()
                qk_slices(do_sgu if full else None)
                if ti + 1 < len(tiles):
                    build_hT(ti + 1)
                do_v()

            stage_begin()
            NP = NT + 6
            OC = 2
            OX = C + 5
            wl = alloc_bf(2 * 2 * 8 * 128, "wl")
            S.dma("sp", lambda e, l=l: e.dma_start(out=wl.ap.rearrange("p (a b) -> p a b", a=32), in_=Wlru_bf[l].rearrange("p d a g j -> p (d a g) j")),
                  writes=[wl.res])
            wl5 = wl.ap.rearrange("p (d a g j) -> p d a g j", d=2, a=2, g=8)
            P = alloc(NP, "P"); xcs = [alloc(NP, "xc%d" % i) for i in range(2)]; xcb = alloc_bf(NP + 2, "xcb")
            Rb = [alloc(NP, "R%d" % i) for i in range(2)]
            Ibs = [alloc(NP, "I%d" % i) for i in range(2)]
            Abs = [alloc(NP, "A%d" % i) for i in range(2)]
            hst = [alloc_bf(NT, "hst%d" % i) for i in range(1)]
            lo, hi = 2, NP - 1
            ctiles = []
            c0 = lo
            while c0 < hi:
                n = min(512, hi - c0)
                ctiles.append((c0, n))
                c0 += n
            def load_conv(cc):
                xc = xcs[cc % 2]
                S.op("pool", lambda e: e.memset(P.ap[:, 0:2], 0.0), writes=[P.res])
                S.op("pool", lambda e: e.memset(P.ap[:, OC + C:OC + C + 3], 0.0), writes=[P.res], partial=[P.res])
                S.op("pool", lambda e: e.memset(P.ap[:, OX + T:OX + T + 1], 0.0), writes=[P.res], partial=[P.res])
                S.dma("sp", lambda e, cc=cc: e.dma_start(out=P.ap[:, OC:OC + C], in_=A_scr[s][cc, :, 0:C]), writes=[P.res], partial=[P.res])
                S.dma("sp", lambda e, cc=cc: e.dma_start(out=P.ap[:, OX:OX + T], in_=A_scr[s][cc, :, C:NT]), writes=[P.res], partial=[P.res])
                wcol = lambda t, cc=cc: fmA.ap[:, 96 + t * 8 + cc: 96 + t * 8 + cc + 1]
                S.op("dve", lambda e, cc=cc, wcol=wcol: e.tensor_scalar(xc.ap[:, lo:hi], P.ap[:, lo:hi], wcol(2), fmB.ap[:, cc:cc + 1], ALU.mult, ALU.add),
                     reads=[P.res, fmA.res, fmB.res], writes=[xc.res])
                for (t, sh_) in ((0, -2), (1, -1), (3, 1)):
                    S.op("dve", lambda e, t=t, sh_=sh_, wcol=wcol: e.scalar_tensor_tensor(
                        xc.ap[:, lo:hi], P.ap[:, lo + sh_:hi + sh_], wcol(t), xc.ap[:, lo:hi], ALU.mult, ALU.add),
                        reads=[P.res, fmA.res, xc.res], writes=[xc.res])

            load_conv(0)
            for cc in range(8):
                xc = xcs[cc % 2]
                S.op("act", lambda e: e.activation(out=xcb.ap[:, lo:hi], in_=xc.ap[:, lo:hi], func=AF.Copy), reads=[xc.res], writes=[xcb.res])
                for dr in range(2):
                    if dr == 1 and cc + 1 < 8:
                        load_conv(cc + 1)
                    R = Rb[dr]
                    Ib = Ibs[dr]
                    Ab = Abs[dr]
                    for (gate, dstt, bcol0) in ((0, R, 8), (1, Ib, 24)):
                        for (c0, n) in ctiles:
                            b = nbank()
                            mm_group(bank(b, n), [(wl5[:, dr, gate, cc, :], xcb.ap[:, c0:c0 + n])], [wl.res, xcb.res], PB[b])
                            bc_ = bcol0 + dr * 8 + cc
                            S.op("act", lambda e, dstt=dstt, b=b, c0=c0, n=n, bc_=bc_: e.activation(
                                out=dstt.ap[:, c0:c0 + n], in_=bank(b, n), func=AF.Sigmoid, bias=fmB.ap[:, bc_:bc_ + 1]),
                                reads=[PB[b], fmB.res], writes=[dstt.res], partial=[dstt.res])
                    ci = dr * 8 + cc
                    S.op("act", lambda e, R=R, ci=ci: e.activation(out=Ab.ap[:, lo:hi], in_=R.ap[:, lo:hi], func=AF.Exp, scale=cpt.ap[:, ci:ci + 1]),
                         reads=[R.res, cpt.res], writes=[Ab.res])
                    S.op("act", lambda e, R=R, ci=ci: e.activation(out=R.ap[:, lo:hi], in_=R.ap[:, lo:hi], func=AF.Exp, scale=cpt.ap[:, 16 + ci:16 + ci + 1]),
                         reads=[R.res, cpt.res], writes=[R.res])
                    S.op("act", lambda e, R=R: e.activation(out=R.ap[:, lo:hi], in_=R.ap[:, lo:hi], func=AF.Sqrt, scale=-1.0, bias=1.0),
                         reads=[R.res], writes=[R.res])
                    S.op("dve", lambda e: e.tensor_tensor(Ib.ap[:, lo:hi], Ib.ap[:, lo:hi], xc.ap[:, lo:hi], ALU.mult),
                         reads=[Ib.res, xc.res], writes=[Ib.res])
                    S.op("dve", lambda e, R=R: e.tensor_tensor(Ib.ap[:, lo:hi], Ib.ap[:, lo:hi], R.ap[:, lo:hi], ALU.mult),
                         reads=[Ib.res, R.res], writes=[Ib.res])
                    if dr == 0:
                        S.op("dve", lambda e, R=R: e.tensor_tensor_scan(R.ap[:, OC:OC + C], Ab.ap[:, OC:OC + C], Ib.ap[:, OC:OC + C], 0.0, ALU.mult, ALU.add),
                             reads=[Ab.res, Ib.res], writes=[R.res])
                        S.op("dve", lambda e, R=R: e.tensor_tensor_scan(R.ap[:, OX:OX + T], Ab.ap[:, OX:OX + T], Ib.ap[:, OX:OX + T],
                                                                        R.ap[:, OC + C - 1:OC + C], ALU.mult, ALU.add),
                             reads=[Ab.res, Ib.res, R.res], writes=[R.res])
                    else:
                        S.op("dve", lambda e, R=R: e.tensor_tensor_scan(rev(R.ap[:, OC:OC + C]), rev(Ab.ap[:, OC:OC + C]), rev(Ib.ap[:, OC:OC + C]),
                                                                        0.0, ALU.mult, ALU.add),
                             reads=[Ab.res, Ib.res], writes=[R.res])
                        S.op("dve", lambda e, R=R: e.tensor_tensor_scan(rev(R.ap[:, OX:OX + T]), rev(Ab.ap[:, OX:OX + T]), rev(Ib.ap[:, OX:OX + T]),
                                                                        R.ap[:, OC:OC + 1], ALU.mult, ALU.add),
                             reads=[Ab.res, Ib.res, R.res], writes=[R.res])
                hs = hst[0]
                S.op("pool", lambda e, hs=hs: e.tensor_tensor(hs.ap[:, 0:C], Rb[0].ap[:, OC:OC + C], Rb[1].ap[:, OC:OC + C], ALU.add),
                     reads=[Rb[0].res, Rb[1].res], writes=[hs.res])
                S.op("pool", lambda e, hs=hs: e.tensor_tensor(hs.ap[:, C:NT], Rb[0].ap[:, OX:OX + T], Rb[1].ap[:, OX:OX + T], ALU.add),
                     reads=[Rb[0].res, Rb[1].res], writes=[hs.res], partial=[hs.res])
                S.dma("pool", lambda e, hs=hs, cc=cc: e.dma_start(out=hA_scr[s][cc], in_=hs.ap), reads=[hs.res])

            stage_begin()
            Vsb = alloc_bf(NCH * D, "Vsb")
            V3 = Vsb.v3(NCH)
            nvp = 4
            per = (NCH + nvp - 1) // nvp
            for i in range(nvp):
                a0, a1 = i * per, min(NCH, (i + 1) * per)
                if a0 >= a1:
                    continue
                S.dma("sp", lambda e, a0=a0, a1=a1: e.dma_start(out=V3[:, a0:a1, :],
                                                                in_=V_scr[s][a0 * 128:a1 * 128, :].rearrange("(c p) e -> p c e", p=128)),
                      writes=[Vsb.res], partial=[Vsb.res])
            kTh = [alloc_bf(NT, "kTh%d" % i) for i in range(2)]
            qTh = [alloc_bf(NT, "qTh%d" % i) for i in range(2)]
            Pt = [alloc_bf(512, "Pt%d" % i) for i in range(8)]
            scnt = [0]
            rs = [alloc(512, "rs%d" % i) for i in range(2)]
            ob32 = alloc(512, "ob32"); t32 = alloc(512, "t32")
            acc = [alloc(512, "acc%d" % i) for i in range(2)]
            ost = [alloc_bf(512, "ost%d" % i) for i in range(2)]
            pcnt = 0
            qcnt = 0
            qtiles = []
            if not last:
                qtiles.append((0, C, 0, C // 128))
            for i in range(T // 512):
                qtiles.append((C + i * 512, 512, 0, NCH))
            for h in range(8):
                kt = kTh[h % 2]; qt = qTh[h % 2]
                S.dma("sp", lambda e, kt=kt, h=h: e.dma_start(out=kt.ap, in_=kT_scr[s][h]), writes=[kt.res])
                S.dma("sp", lambda e, qt=qt, h=h: e.dma_start(out=qt.ap, in_=qT_scr[s][h]), writes=[qt.res])
                for (q0, NQ, kc0, kc1) in qtiles:
                    nk = kc1 - kc0
                    pendq = []
                    LEAD = 2
                    for ki in range(nk + LEAD):
                        if ki < nk:
                            kc = kc0 + ki
                            pts = []
                            for m in range(2):
                                b = scnt[0] % 5; scnt[0] += 1
                                mm_group(bank(b, NQ), [(kt.ap[m * 64:(m + 1) * 64, kc * 128:(kc + 1) * 128], qt.ap[m * 64:(m + 1) * 64, q0:q0 + NQ])],
                                         [kt.res, qt.res], PB[b])
                                pt = Pt[pcnt % len(Pt)]; pcnt += 1
                                S.op("act", lambda e, pt=pt, b=b, NQ=NQ: e.activation(out=pt.ap[:, 0:NQ], in_=bank(b, NQ), func=AF.Exp, scale=0.125),
                                     reads=[PB[b]], writes=[pt.res])
                                pts.append(pt)
                            pendq.append((kc, pts, ki))
                        if ki >= LEAD:
                            kcp, ptsp, kip = pendq.pop(0)
                            first = (kip == 0)
                            lastk = (kip == nk - 1)
                            pt = ptsp[0]

                            def pvf0(e, pt=pt, first=first, lastk=lastk, kcp=kcp, h=h, NQ=NQ):
                                e.matmul(bank(5, NQ), V3[:, kcp, h * 128:(h + 1) * 128], pt.ap[:, 0:NQ], start=first, stop=lastk)
                                return e.matmul(bank(7, NQ), onesb.ap, pt.ap[:, 0:NQ], start=first, stop=lastk)
                            S.op("pe", pvf0, reads=[pt.res, Vsb.res, onesb.res], writes=[PB[5], PB[7]],
                                 partial=([] if first else [PB[5], PB[7]]))
                            pt = ptsp[1]

                            def pvf1(e, pt=pt, first=first, lastk=lastk, kcp=kcp, h=h, NQ=NQ):
                                return e.matmul(bank(6, NQ), V3[:, kcp, h * 128:(h + 1) * 128], pt.ap[:, 0:NQ], start=first, stop=lastk)
                            S.op("pe", pvf1, reads=[pt.res, Vsb.res], writes=[PB[6]], partial=([] if first else [PB[6]]))
                            ac = acc[1]
                            if first:
                                S.op("dve", lambda e, ac=ac, pt=pt, NQ=NQ: e.tensor_copy(ac.ap[:, 0:NQ], pt.ap[:, 0:NQ]),
                                     reads=[pt.res], writes=[ac.res])
                            else:
                                S.op("dve", lambda e, ac=ac, pt=pt, NQ=NQ: e.tensor_tensor(ac.ap[:, 0:NQ], ac.ap[:, 0:NQ], pt.ap[:, 0:NQ], ALU.add),
                                     reads=[pt.res, ac.res], writes=[ac.res])
                    bx = scnt[0] % 5; scnt[0] += 1
                    mm_group(bank(bx, NQ), [(onesf.ap, acc[1].ap[:, 0:NQ])], [onesf.res, acc[1].res], PB[bx])
                    for (m, bsum) in ((0, 7), (1, bx)):
                        S.op("act", lambda e, m=m, bsum=bsum, NQ=NQ: e.activation(out=rs[m].ap[:, 0:NQ], in_=bank(bsum, NQ), func=AF.Ln),
                             reads=[PB[bsum]], writes=[rs[m].res])
                        S.op("act", lambda e, m=m, NQ=NQ: e.activation(out=rs[m].ap[:, 0:NQ], in_=rs[m].ap[:, 0:NQ], func=AF.Exp, scale=-1.0),
                             reads=[rs[m].res], writes=[rs[m].res])
                    S.op("dve", lambda e, NQ=NQ: e.tensor_tensor(ob32.ap[:, 0:NQ], bank(5, NQ), rs[0].ap[:, 0:NQ], ALU.mult),
                         reads=[PB[5], rs[0].res], writes=[ob32.res])
                    S.op("dve", lambda e, NQ=NQ: e.tensor_tensor(t32.ap[:, 0:NQ], bank(6, NQ), rs[1].ap[:, 0:NQ], ALU.mult),
                         reads=[PB[6], rs[1].res], writes=[t32.res])
                    os_ = ost[qcnt % 2]; qcnt += 1
                    S.op("dve", lambda e, os_=os_, NQ=NQ: e.scalar_tensor_tensor(os_.ap[:, 0:NQ], t32.ap[:, 0:NQ], small.ap[:, 4:5], ob32.ap[:, 0:NQ],
                                                                                 ALU.mult, ALU.add),
                         reads=[t32.res, ob32.res, small.res], writes=[os_.res])
                    S.dma("pool", lambda e, os_=os_, h=h, q0=q0, NQ=NQ: e.dma_start(out=at_scr[s][h, :, q0:q0 + NQ], in_=os_.ap[:, 0:NQ]), reads=[os_.res])

            stage_begin()
            gt_s = alloc(D, "gt_s")
            make_bc(gt_s, s, 2)
            gt_c = None
            if not last:
                gt_c = alloc(D, "gt_c")
                make_bc(gt_c, NSEQ, 2)
            lng = alloc(D, "lng3"); lnb = alloc(D, "lnb3")
            bcast_load(lng, ln_g[l]); bcast_load(lnb, ln_b[l])
            hT = [alloc_bf(8 * 512, "hT3_%d" % i) for i in range(2)]
            wbuf = [alloc_bf(8 * 512, "wbuf3_%d" % i) for i in range(3)]
            G = alloc_bf(8 * 512, "G")
            Bt = [alloc_bf(8 * 512, "Bt%d" % i) for i in range(2)]
            Sg = alloc_bf(8 * 512, "Sg")
            M = alloc(8 * 512, "M")
            sq = [alloc_bf(512, "sq%d" % i) for i in range(2)]
            rt8 = alloc(8 * 512, "rt8")
            rt8_3 = rt8.v3(8)
            tmpm = [alloc(512, "tmpm%d" % i) for i in range(2)]
            xres = [alloc(D, "xres%d" % i) for i in range(4)]
            ostt = [alloc(D, "ostt%d" % i) for i in range(2)]
            stats = alloc(12, "stats3"); mv = alloc(2, "mv3"); rstd = alloc(2, "rstd3")
            cnt = {"w": 0, "b": 0, "r": 0, "t": 0, "x": 0}
            br_src = [hA_scr, at_scr, sg_scr]
            gate_sl = [1, 5, 8]
            s3tiles = tiles if not last else tiles[1:]
            pend_tail = [None]
            for ti, (tok0, N, is_ctx) in enumerate(s3tiles):
                NS = N // 128
                hTt = hT[ti % 2]
                hT3 = hTt.v3(8)
                S.dma("sp", lambda e, hT3=hT3, tok0=tok0, N=N: e.dma_start(out=hT3[:, :, 0:N], in_=hT_scr[s][:, :, tok0:tok0 + N]), writes=[hTt.res])
                G3 = G.v3(8); Sg3 = Sg.v3(8); M3 = M.v3(8)

                def load_w3(src_ap):
                    wb = wbuf[cnt["w"] % 3]; cnt["w"] += 1
                    S.dma("sp", lambda e, wb=wb, src_ap=src_ap: e.dma_start(out=wb.v3(8), in_=src_ap), writes=[wb.res])
                    return wb

                def gate_slice(sl, dst, dst3, func):
                    for piece in range(2):
                        wb = load_w3(Win_bf[l][:, :, sl * D + piece * 512: sl * D + (piece + 1) * 512])
                        w3 = wb.v3(8)
                        for ec in range(4):
                            c = piece * 4 + ec
                            b = nbank()
                            mm_group(bank(b, N), [(w3[:, k, ec * 128:(ec + 1) * 128], hT3[:, k, 0:N]) for k in range(8)],
                                     [wb.res, hTt.res], PB[b])
                            bcol = sl * 8 + c
                            S.op("act", lambda e, c=c, b=b, bcol=bcol, dst3=dst3, func=func: e.activation(
                                out=dst3[:, c, 0:N], in_=bank(b, N), func=func, bias=fmA.ap[:, bcol:bcol + 1]),
                                reads=[PB[b], fmA.res], writes=[dst.res], partial=[dst.res])

                for j in range(3):
                    if j == 1:
                        for sub in range(NS):
                            xr = xres[sub]
                            r0 = (tok0 if is_ctx else tok0 - C) + sub * 128
                            if l == 0:
                                src = ctx_d[s, r0:r0 + 128, :] if is_ctx else x_d[s, r0:r0 + 128, :]
                            else:
                                src = c1_scr[s, r0:r0 + 128, :] if is_ctx else x1_scr[s, r0:r0 + 128, :]
                            S.dma("sp", lambda e, xr=xr, src=src: e.dma_start(out=xr.ap, in_=src), writes=[xr.res])
                    Bj = Bt[cnt["b"] % 2]; cnt["b"] += 1
                    B3 = Bj.v3(8)
                    S.dma("sp", lambda e, B3=B3, j=j, tok0=tok0, N=N: e.dma_start(
                        out=B3[:, :, 0:N], in_=br_src[j][s].rearrange("c p t -> p c t")[:, :, tok0:tok0 + N]), writes=[Bj.res])
                    if j == 1:
                        for c in range(8):
                            i2 = cnt["r"] % 2; cnt["r"] += 1
                            sqt = sq[i2]
                            S.op("pool", lambda e, sqt=sqt, B3=B3, c=c: e.tensor_tensor(sqt.ap[:, 0:N], B3[:, c, 0:N], B3[:, c, 0:N], ALU.mult),
                                 reads=[Bj.res], writes=[sqt.res])
                            b = nbank()
                            mm_group(bank(b, N), [(onesm.ap, sqt.ap[:, 0:N])], [onesm.res, sqt.res], PB[b])
                            S.op("act", lambda e, b=b, c=c: e.activation(out=rt8_3[:, c, 0:N], in_=bank(b, N), func=AF.Ln, bias=float(RMS_EPS)),
                                 reads=[PB[b]], writes=[rt8.res], partial=[rt8.res])
                        S.op("act", lambda e: e.activation(out=rt8_3[:, :, 0:N], in_=rt8_3[:, :, 0:N], func=AF.Exp, scale=-0.5), reads=[rt8.res], writes=[rt8.res])
                        S.op("pool", lambda e, B3=B3: e.tensor_tensor(rt8_3[:, :, 0:N], B3[:, :, 0:N], rt8_3[:, :, 0:N], ALU.mult),
                             reads=[Bj.res, rt8.res], writes=[rt8.res])
                    gate_slice(gate_sl[j], G, G3, AF.Silu)
                    gate_slice(9 + j, Sg, Sg3, AF.Sigmoid)
                    if j == 0 and pend_tail[0] is not None:
                        pend_tail[0]()
                        pend_tail[0] = None
                    if j == 1:
                        for hf in range(2):
                            S.op("dve", lambda e, hf=hf: e.scalar_tensor_tensor(G3[:, hf * 4:(hf + 1) * 4, 0:N], rt8_3[:, hf * 4:(hf + 1) * 4, 0:N], small.ap[:, 5:6],
                                                                                 G3[:, hf * 4:(hf + 1) * 4, 0:N], ALU.mult, ALU.mult),
                                 reads=[rt8.res, small.res, G.res], writes=[G.res], partial=[G.res])
                    else:
                        S.op("pool", lambda e, B3=B3: e.tensor_tensor(G3[:, :, 0:N], B3[:, :, 0:N], G3[:, :, 0:N], ALU.mult),
                             reads=[Bj.res, G.res], writes=[G.res])
                    for piece in range(2):
                        wb = load_w3(Wbr_bf[l][:, j, :, piece * 512:(piece + 1) * 512])
                        w3 = wb.v3(8)
                        for ec in range(4):
                            c = piece * 4 + ec
                            b = nbank()
                            mm_group(bank(b, N), [(w3[:, k, ec * 128:(ec + 1) * 128], G3[:, k, 0:N]) for k in range(8)],
                                     [wb.res, G.res], PB[b])
                            if j == 0:
                                S.op("dve", lambda e, c=c, b=b: e.tensor_tensor(M3[:, c, 0:N], bank(b, N), Sg3[:, c, 0:N], ALU.mult),
                                     reads=[PB[b], Sg.res], writes=[M.res], partial=[M.res])
                            else:
                                tm_ = tmpm[cnt["t"] % 2]; cnt["t"] += 1
                                S.op("dve", lambda e, c=c, b=b, tm_=tm_: e.tensor_tensor(tm_.ap[:, 0:N], bank(b, N), Sg3[:, c, 0:N], ALU.mult),
                                     reads=[PB[b], Sg.res], writes=[tm_.res])
                                S.op("pool", lambda e, c=c, tm_=tm_: e.tensor_tensor(M3[:, c, 0:N], M3[:, c, 0:N], tm_.ap[:, 0:N], ALU.add),
                                     reads=[tm_.res, M.res], writes=[M.res])
                S.op("act", lambda e: e.activation(out=G3[:, :, 0:N], in_=M3[:, :, 0:N], func=AF.Copy), reads=[M.res], writes=[G.res])
                yg = M.ap.rearrange("p (s e) -> p s e", s=4)
                gt = gt_c if is_ctx else gt_s
                for half in range(2):
                    wb = load_w3(Wout_bf[l][:, :, half * 512:(half + 1) * 512])
                    w3 = wb.v3(8)
                    for sub in range(NS):
                        b = nbank()
                        mm_group(bank(b, 512), [(G3[:, k, sub * 128:(sub + 1) * 128], w3[:, k, :]) for k in range(8)],
                                 [wb.res, G.res], PB[b])
                        S.op("dve", lambda e, sub=sub, half=half, b=b, gt=gt: e.tensor_tensor(
                            yg[:, sub, half * 512:(half + 1) * 512], bank(b, 512), gt.ap[:, half * 512:(half + 1) * 512], ALU.mult),
                            reads=[PB[b], gt.res], writes=[M.res], partial=[M.res])
                def make_tail(tok0=tok0, is_ctx=is_ctx, NS=NS, yg=yg):
                    def tail():
                        for sub in range(NS):
                            xr = xres[sub]; ot = ostt[cnt["x"] % 2]; cnt["x"] += 1
                            r0 = (tok0 if is_ctx else tok0 - C) + sub * 128
                            S.op("dve", lambda e, xr=xr, sub=sub: e.scalar_tensor_tensor(xr.ap, xr.ap, float(alpha), yg[:, sub, :], ALU.mult, ALU.add),
                                 reads=[xr.res, M.res], writes=[xr.res])
                            layer_norm_rows((xr.ap, xr.res), (ot.ap, ot.res), lng, lnb, stats, mv, rstd, LN_EPS)
                            if last:
                                dst = out_d[s, r0:r0 + 128, :]
                            else:
                                dst = c1_scr[s, r0:r0 + 128, :] if is_ctx else x1_scr[s, r0:r0 + 128, :]
                            S.dma("pool", lambda e, ot=ot, dst=dst: e.dma_start(out=dst, in_=ot.ap), reads=[ot.res])
                    return tail
                pend_tail[0] = make_tail()
            if pend_tail[0] is not None:
                pend_tail[0]()
                pend_tail[0] = None

    S.barrier()
    gens = S.sem_gens()
    sems = {e: [es.enter_context(nc.semaphore("c_%s%d" % (e, i))) for i in range(gens[e])] for e in ENGS}
    dsems = {e: [es.enter_context(nc.semaphore("d_%s%d" % (e, i))) for i in range(NDMASEM)] for e in ("sp", "pool", "act")}
    block = es.enter_context(nc.Block())
    S.emit(block, sems, dsems)
    es.close()
    return nc


def rope_tables(T):
    GRID_W = 64
    nf = 16
    t = np.arange(T)
    row = (t // GRID_W).astype(np.float32)
    col = (t % GRID_W).astype(np.float32)
    freqs = (np.float32(10000.0) ** (-np.arange(nf, dtype=np.float32) / np.float32(nf))).astype(np.float32)
    cosT = np.zeros((128, T), np.float32)
    sinT = np.zeros((128, T), np.float32)
    for p in range(128):
        dd = p % 64
        a = dd // 32
        bsel = (dd % 32) // 16
        f = dd % 16
        ang = (row if a == 0 else col) * freqs[f]
        cosT[p] = np.cos(ang)
        sinT[p] = np.sin(ang) * (-1.0 if bsel == 0 else 1.0)
    return cosT, sinT


def rope_perm():
    pm = np.zeros((128, 128), np.float32)
    for m in range(128):
        dd = m % 64
        bsel = (dd % 32) // 16
        partner = m + 16 if bsel == 0 else m - 16
        pm[partner, m] = 1.0
    return pm


def const_inputs(T, NV):
    cosT, sinT = rope_tables(T)
    sel = np.zeros((NV, NV * 128), np.float32)
    for v in range(NV):
        sel[v, v * 128:(v + 1) * 128] = 1.0
    return {
        "identb": np.eye(128, dtype=np.float32).astype(ml_dtypes.bfloat16),
        "identf": np.eye(128, dtype=np.float32),
        "permf": rope_perm(),
        "cosT": cosT,
        "sinT": sinT,
        "sel": sel,
    }


_WNAMES = ["w_ada", "b_ada", "w_in", "b_in", "conv_w", "conv_b", "lru_wa", "lru_ba", "lru_wx", "lru_bx",
           "lru_lambda", "diff_lambda", "diff_norm_g", "sgu_ln_g", "sgu_ln_b", "sgu_w", "sgu_b",
           "w_branch", "w_out", "ln_g", "ln_b"]


def kernel(**inputs):
    x = np.ascontiguousarray(np.asarray(inputs["x"], dtype=np.float32))
    ctx = np.ascontiguousarray(np.asarray(inputs["ctx"], dtype=np.float32))
    c = np.asarray(inputs["c"], dtype=np.float32)
    c_ctx = np.asarray(inputs["c_ctx"], dtype=np.float32)
    B, T, _ = x.shape
    C = ctx.shape[1]
    L = inputs["w_in"].shape[0]
    ncores = 8
    NSEQ = B // ncores
    nc = build_program(T, C, L, NSEQ)
    consts = const_inputs(T, NSEQ + 1)
    wts = {k: np.ascontiguousarray(np.asarray(inputs[k], dtype=np.float32)) for k in _WNAMES}
    in_maps = []
    for i in range(ncores):
        m = {"x": x[i * NSEQ:(i + 1) * NSEQ], "ctx": ctx[i * NSEQ:(i + 1) * NSEQ],
             "cvec": np.ascontiguousarray(np.concatenate([c[i * NSEQ:(i + 1) * NSEQ], c_ctx[None, :]], axis=0))}
        m.update(wts)
        m.update(consts)
        in_maps.append(m)
    res = run_bass_kernel_spmd(nc, in_maps, core_ids=list(range(ncores)))
    return np.concatenate([np.asarray(r["out"]) for r in res.results], axis=0).astype(np.float32)
```
